# Optimizing a Trainium2 kernel written in Bass

```python
import math
import jax, jax.numpy as jnp
from jax import lax
import numpy as np

D_MODEL = 1024
BATCH = 4
SEQ = 4096
DEPTH = 2

HEAD_DIM = 64
ROT_DIM = HEAD_DIM // 4
ROPE_THETA = 500000.0
Q_BLOCK = 128
EPS = 1e-6
NEG = -1e30
PLE_DIM = 256

DA_HEADS = 4
DA_QK = DA_HEADS * 2 * HEAD_DIM
DA_VDIM = 2 * HEAD_DIM
DA_WIDTH = DA_HEADS * DA_VDIM

DSA_HEADS = 8
DSA_WIDTH = DSA_HEADS * HEAD_DIM
IDX_HEADS = 8
IDX_DIM = 64
DSA_TOPK_MAX = 256

NSA_GROUPS = 2
NSA_HPG = 4
NSA_HEADS = NSA_GROUPS * NSA_HPG
NSA_WIDTH = NSA_HEADS * HEAD_DIM
KV_C = NSA_GROUPS * HEAD_DIM
CMP_LEN = 32
CMP_STRIDE = 16
CMP_HIDDEN = 128
SLC_BLOCK = 64
SLC_TOPN_MAX = 16
WINDOW = 512
FORCE_SCORE = 1e4

BRANCH_WIDTH = 512
N_BRANCH = 3

IN_SPLITS = (
    DA_QK, DA_QK, DA_WIDTH, BRANCH_WIDTH,
    DSA_WIDTH, HEAD_DIM, HEAD_DIM, BRANCH_WIDTH,
    IDX_HEADS * IDX_DIM, IDX_DIM, IDX_HEADS,
    NSA_WIDTH, KV_C, KV_C, KV_C, KV_C, KV_C, KV_C,
    BRANCH_WIDTH, NSA_HEADS * 3,
    N_BRANCH * D_MODEL,
)
N_IN = sum(IN_SPLITS)

kernel_name = "hybrid_diff_dsa_nsa_trunk"


def rmsnorm(x, g):
    xf = x.astype(jnp.float32)
    y = xf * lax.rsqrt(jnp.mean(xf * xf, axis=-1, keepdims=True) + EPS)
    return (y * g.astype(jnp.float32)).astype(x.dtype)


def masked_softmax(s, mask):
    s = jnp.where(mask, s.astype(jnp.float32), NEG)
    return jnp.where(mask, jax.nn.softmax(s, axis=-1), 0.0)


def rope_tables(positions):
    inv = ROPE_THETA ** (-jnp.arange(0, ROT_DIM, 2, dtype=jnp.float32) / ROT_DIM)
    ang = positions.astype(jnp.float32)[..., None] * inv
    return jnp.cos(ang), jnp.sin(ang)


def apply_rope(x, cos, sin):
    shp = cos.shape[:2] + (1,) * (x.ndim - 3) + cos.shape[2:]
    c = cos.reshape(shp).astype(x.dtype)
    s = sin.reshape(shp).astype(x.dtype)
    half = ROT_DIM // 2
    x1, x2 = x[..., :half], x[..., half:ROT_DIM]
    return jnp.concatenate([x1 * c - x2 * s, x2 * c + x1 * s, x[..., ROT_DIM:]], axis=-1)


def unblock(o):
    nb, b, qb = o.shape[:3]
    return jnp.moveaxis(o, 0, 1).reshape(b, nb * qb, -1)


def diff_attention(q, k, v, lam, subln_g, lam_init):
    S = q.shape[1]
    kpos = jnp.arange(S)
    scale = HEAD_DIM ** -0.5

    def block(i):
        q0 = i * Q_BLOCK
        tpos = q0 + jnp.arange(Q_BLOCK)
        qb = lax.dynamic_slice_in_dim(q, q0, Q_BLOCK, axis=1)
        s = jnp.einsum('bqhcd,bkhcd->bhcqk', qb, k).astype(jnp.float32) * scale
        pr = masked_softmax(s, kpos[None, :] <= tpos[:, None])
        a = pr[:, :, 0] - lam * pr[:, :, 1]
        return jnp.einsum('bhqk,bkhe->bqhe', a.astype(v.dtype), v)

    o = lax.map(block, jnp.arange(S // Q_BLOCK))
    b = q.shape[0]
    o = jnp.moveaxis(o, 0, 1).reshape(b, S, DA_HEADS, DA_VDIM)
    o = rmsnorm(o, subln_g) * (1.0 - lam_init)
    return o.reshape(b, S, DA_WIDTH)


def dsa_attention(q, k, v, iq, ik, iw, k_sel):
    S = q.shape[1]
    kpos = jnp.arange(S)
    scale = HEAD_DIM ** -0.5
    gather = jax.vmap(lambda kk, ii: kk[ii])

    def block(i):
        q0 = i * Q_BLOCK
        tpos = q0 + jnp.arange(Q_BLOCK)
        qb = lax.dynamic_slice_in_dim(q, q0, Q_BLOCK, axis=1)
        iqb = lax.dynamic_slice_in_dim(iq, q0, Q_BLOCK, axis=1)
        iwb = lax.dynamic_slice_in_dim(iw, q0, Q_BLOCK, axis=1).astype(jnp.float32) * IDX_HEADS ** -0.5
        dots = jnp.einsum('bqhd,bkd->bqhk', iqb, ik).astype(jnp.float32) * IDX_DIM ** -0.5
        score = jnp.einsum('bqh,bqhk->bqk', iwb, jax.nn.relu(dots))
        score = jnp.where(kpos[None, :] <= tpos[:, None], score, -jnp.inf)
        _, idx = lax.top_k(score, k_sel)
        valid = idx <= tpos[None, :, None]
        ksel = gather(k, idx)
        vsel = gather(v, idx)
        s = jnp.einsum('bqhd,bqkd->bqhk', qb, ksel).astype(jnp.float32) * scale
        pr = masked_softmax(s, valid[:, :, None, :])
        return jnp.einsum('bqhk,bqkd->bqhd', pr.astype(v.dtype), vsel)

    return unblock(lax.map(block, jnp.arange(S // Q_BLOCK)))


def compress(kv, pe, w1, w2):
    S = kv.shape[1]
    nc = (S - CMP_LEN) // CMP_STRIDE + 1
    idx = np.arange(nc)[:, None] * CMP_STRIDE + np.arange(CMP_LEN)[None, :]
    blocks = kv[:, idx] + pe[None, None, :, None, :]
    hid = jax.nn.gelu(jnp.einsum('bnlgd,ldf->bngf', blocks, w1))
    return jnp.einsum('bngf,fd->bngd', hid, w2)


def nsa_attention(q, q_rot, kc, vc, ks, vs, kw, vw, gates, n_sel):
    B, S = q.shape[:2]
    nc = kc.shape[1]
    n_blk = S // SLC_BLOCK
    scale = HEAD_DIM ** -0.5
    cstart = np.arange(nc) * CMP_STRIDE
    sstart = np.arange(n_blk) * SLC_BLOCK
    overlap = jnp.asarray(((cstart[:, None] < sstart[None, :] + SLC_BLOCK)
                           & (cstart[:, None] + CMP_LEN > sstart[None, :])).astype(np.float32))
    cmp_end = jnp.arange(nc) * CMP_STRIDE + CMP_LEN - 1
    blk = jnp.arange(n_blk)
    ks_blk = ks.reshape(B, n_blk, SLC_BLOCK, NSA_GROUPS, HEAD_DIM).transpose(0, 3, 1, 2, 4)
    vs_blk = vs.reshape(B, n_blk, SLC_BLOCK, NSA_GROUPS, HEAD_DIM).transpose(0, 3, 1, 2, 4)
    kw_pad = jnp.pad(kw, ((0, 0), (WINDOW, 0), (0, 0), (0, 0)))
    vw_pad = jnp.pad(vw, ((0, 0), (WINDOW, 0), (0, 0), (0, 0)))
    gather = jax.vmap(jax.vmap(lambda kb, ii: kb[ii]))

    def block(i):
        q0 = i * Q_BLOCK
        tpos = q0 + jnp.arange(Q_BLOCK)
        qb = lax.dynamic_slice_in_dim(q, q0, Q_BLOCK, axis=1)
        qrb = lax.dynamic_slice_in_dim(q_rot, q0, Q_BLOCK, axis=1)
        gb = jax.nn.sigmoid(lax.dynamic_slice_in_dim(gates, q0, Q_BLOCK, axis=1))
        sc = jnp.einsum('bqghd,bngd->bghqn', qb, kc).astype(jnp.float32) * scale
        pc = masked_softmax(sc, cmp_end[None, :] <= tpos[:, None])
        o_cmp = jnp.einsum('bghqn,bngd->bqghd', pc.astype(vc.dtype), vc)
        imp = jnp.einsum('bghqn,nm->bgqm', pc, overlap)
        forced = (blk[None, :] == (tpos // SLC_BLOCK)[:, None]) | (blk[None, :] == 0)
        imp = jnp.where(forced, FORCE_SCORE, imp)
        imp = jnp.where(blk[None, :] * SLC_BLOCK <= tpos[:, None], imp, -jnp.inf)
        _, sel = lax.top_k(imp, n_sel)
        ksel = gather(ks_blk, sel)
        vsel = gather(vs_blk, sel).reshape(B, NSA_GROUPS, Q_BLOCK, n_sel * SLC_BLOCK, HEAD_DIM)
        spos = sel[..., None] * SLC_BLOCK + jnp.arange(SLC_BLOCK)
        smask = (spos <= tpos[:, None, None]).reshape(B, NSA_GROUPS, Q_BLOCK, n_sel * SLC_BLOCK)
        ss = jnp.einsum('bqghd,bgqnjd->bghqnj', qrb, ksel).astype(jnp.float32) * scale
        ps = masked_softmax(ss.reshape(B, NSA_GROUPS, NSA_HPG, Q_BLOCK, n_sel * SLC_BLOCK), smask[:, :, None])
        o_slc = jnp.einsum('bghqk,bgqkd->bqghd', ps.astype(vsel.dtype), vsel)
        kwb = lax.dynamic_slice_in_dim(kw_pad, q0, Q_BLOCK + WINDOW, axis=1)
        vwb = lax.dynamic_slice_in_dim(vw_pad, q0, Q_BLOCK + WINDOW, axis=1)
        wpos = q0 - WINDOW + jnp.arange(Q_BLOCK + WINDOW)
        wmask = ((wpos[None, :] <= tpos[:, None]) & (wpos[None, :] > tpos[:, None] - WINDOW)
                 & (wpos[None, :] >= 0))
        sw = jnp.einsum('bqghd,bkgd->bghqk', qrb, kwb).astype(jnp.float32) * scale
        pw = masked_softmax(sw, wmask)
        o_win = jnp.einsum('bghqk,bkgd->bqghd', pw.astype(vwb.dtype), vwb)
        return gb[..., 0:1] * o_cmp + gb[..., 1:2] * o_slc + gb[..., 2:3] * o_win

    return unblock(lax.map(block, jnp.arange(S // Q_BLOCK)))


def setup_inputs(seed: int = 0) -> dict:
    key = jax.random.key(seed)
    ks = jax.random.split(key, 20)

    def nrm(k, shape, scale):
        return jax.random.normal(k, shape, jnp.float32) * scale

    L, D = DEPTH, D_MODEL
    return {
        "x": nrm(ks[0], (BATCH, SEQ, D), 1.0),
        "p": nrm(ks[1], (DEPTH, BATCH, SEQ, PLE_DIM), 1.0),
        "positions": jnp.broadcast_to(jnp.arange(SEQ, dtype=jnp.int32)[None, :], (BATCH, SEQ)),
        "norm_g": 1.0 + nrm(ks[2], (L, D), 0.02),
        "w_in": nrm(ks[3], (L, D, N_IN), D ** -0.5),
        "diff_lambda": nrm(ks[4], (L, 4, HEAD_DIM), 0.1),
        "diff_subln_g": 1.0 + nrm(ks[5], (L, DA_VDIM), 0.02),
        "cmp_pe_k": nrm(ks[6], (L, CMP_LEN, HEAD_DIM), 0.1),
        "cmp_w1_k": nrm(ks[7], (L, CMP_LEN, HEAD_DIM, CMP_HIDDEN), (CMP_LEN * HEAD_DIM) ** -0.5),
        "cmp_w2_k": nrm(ks[8], (L, CMP_HIDDEN, HEAD_DIM), CMP_HIDDEN ** -0.5),
        "cmp_pe_v": nrm(ks[9], (L, CMP_LEN, HEAD_DIM), 0.1),
        "cmp_w1_v": nrm(ks[10], (L, CMP_LEN, HEAD_DIM, CMP_HIDDEN), (CMP_LEN * HEAD_DIM) ** -0.5),
        "cmp_w2_v": nrm(ks[11], (L, CMP_HIDDEN, HEAD_DIM), CMP_HIDDEN ** -0.5),
        "w_up": nrm(ks[12], (L, N_BRANCH, BRANCH_WIDTH, D), BRANCH_WIDTH ** -0.5),
        "w_out": nrm(ks[13], (L, D, D), D ** -0.5),
        "ple_norm_g": 1.0 + nrm(ks[14], (L, D), 0.02),
        "w_ple": nrm(ks[15], (L, PLE_DIM, D), PLE_DIM ** -0.5),
        "w_ple_gate": nrm(ks[16], (L, D, D), D ** -0.5),
        "final_norm_g": 1.0 + nrm(ks[17], (D,), 0.02),
    }


def reference(x, p, positions, norm_g, w_in, diff_lambda, diff_subln_g,
              cmp_pe_k, cmp_w1_k, cmp_w2_k, cmp_pe_v, cmp_w1_v, cmp_w2_v,
              w_up, w_out, ple_norm_g, w_ple, w_ple_gate, final_norm_g):
    B, S, D = x.shape
    k_sel = min(DSA_TOPK_MAX, S // 4)
    n_sel = min(SLC_TOPN_MAX, S // SLC_BLOCK)
    cos, sin = rope_tables(positions)
    offsets = np.cumsum(IN_SPLITS)[:-1].tolist()
    for i in range(DEPTH):
        h = rmsnorm(x, norm_g[i])
        z = h @ w_in[i]
        (a_q, a_k, a_v, a_g, b_q, b_k, b_v, b_g, i_q, i_k, i_w,
         c_q, c_kc, c_vc, c_ks, c_vs, c_kw, c_vw, c_g, c_bg, merge) = jnp.split(z, offsets, axis=-1)

        lam_init = 0.8 - 0.6 * math.exp(-0.3 * i)
        lp = diff_lambda[i].astype(jnp.float32)
        lam = jnp.exp(jnp.sum(lp[0] * lp[1])) - jnp.exp(jnp.sum(lp[2] * lp[3])) + lam_init
        qa = apply_rope(a_q.reshape(B, S, DA_HEADS, 2, HEAD_DIM), cos, sin)
        ka = apply_rope(a_k.reshape(B, S, DA_HEADS, 2, HEAD_DIM), cos, sin)
        o_a = diff_attention(qa, ka, a_v.reshape(B, S, DA_HEADS, DA_VDIM), lam, diff_subln_g[i], lam_init)

        o_b = dsa_attention(apply_rope(b_q.reshape(B, S, DSA_HEADS, HEAD_DIM), cos, sin),
                            apply_rope(b_k, cos, sin), b_v,
                            apply_rope(i_q.reshape(B, S, IDX_HEADS, IDX_DIM), cos, sin),
                            apply_rope(i_k, cos, sin), i_w, k_sel)

        grp = (B, S, NSA_GROUPS, HEAD_DIM)
        qc = c_q.reshape(B, S, NSA_GROUPS, NSA_HPG, HEAD_DIM)
        kc = compress(c_kc.reshape(grp), cmp_pe_k[i], cmp_w1_k[i], cmp_w2_k[i])
        vc = compress(c_vc.reshape(grp), cmp_pe_v[i], cmp_w1_v[i], cmp_w2_v[i])
        o_c = nsa_attention(qc, apply_rope(qc, cos, sin), kc, vc,
                            apply_rope(c_ks.reshape(grp), cos, sin), c_vs.reshape(grp),
                            apply_rope(c_kw.reshape(grp), cos, sin), c_vw.reshape(grp),
                            c_bg.reshape(B, S, NSA_GROUPS, NSA_HPG, 3), n_sel)

        branches = jnp.stack([o_a * jax.nn.silu(a_g), o_b * jax.nn.silu(b_g),
                              o_c * jax.nn.silu(c_g)], axis=2)
        up = jnp.einsum('bsnw,nwd->bsnd', branches, w_up[i])
        gates = jax.nn.sigmoid(merge.reshape(B, S, N_BRANCH, D))
        x = x + jnp.sum(gates * up, axis=2) @ w_out[i]

        x = x + (p[i] @ w_ple[i]) * jax.nn.sigmoid(rmsnorm(x, ple_norm_g[i]) @ w_ple_gate[i])
    return rmsnorm(x, final_norm_g)
```

```python
import math
from contextlib import ExitStack
import numpy as np
import concourse.bass as bass
import concourse.mybir as mybir
from concourse.bass_utils import run_bass_kernel_spmd

F32 = mybir.dt.float32
BF16 = mybir.dt.bfloat16
I32 = mybir.dt.int32
ALU = mybir.AluOpType
AF = mybir.ActivationFunctionType
AX = mybir.AxisListType

P = 128
D = 1024
KC = 8
SEQ = 4096
NB = 32
EPS = 1e-6
NEGM = -30000.0
IN_SPLITS = (512, 512, 512, 512, 512, 64, 64, 512, 512, 64, 8, 512, 128, 128, 128, 128, 128, 128,
             512, 24, 3072)
IN_NAMES = ("a_q", "a_k", "a_v", "a_g", "b_q", "b_k", "b_v", "b_g", "i_q", "i_k", "i_w", "c_q",
            "c_kc", "c_vc", "c_ks", "c_vs", "c_kw", "c_vw", "c_g", "c_bg", "merge")
KV_ORDER = ("a_k", "b_k", "i_k", "c_ks", "c_kw", "a_v", "b_v", "c_vs", "c_vw", "c_kc", "c_vc")
Q_ORDER = ("a_q", "b_q", "i_q", "c_q", "a_g", "b_g", "c_g", "merge", "i_w", "c_bg")


def _col_layout(order):
    offs = np.concatenate([[0], np.cumsum(IN_SPLITS)])
    src = {n: (int(offs[i]), int(offs[i + 1])) for i, n in enumerate(IN_NAMES)}
    idx = []
    pos = {}
    c = 0
    for n in order:
        a, b = src[n]
        idx.append(np.arange(a, b))
        pos[n] = (c, c + b - a)
        c += b - a
    return np.concatenate(idx), pos, c


KV_IDX, KVP, NKV = _col_layout(KV_ORDER)
Q_IDX, QP, NQC = _col_layout(Q_ORDER)


class Buf:
    __slots__ = ("name", "w", "r")

    def __init__(self, name=""):
        self.name = name
        self.w = {}
        self.r = {}


class Eng:
    def __init__(self, name, h, sem):
        self.name = name
        self.h = h
        self.sem = sem
        self.count = 0
        self.seen = {}


class TT:
    def __init__(self, t, name):
        self.t = t
        self.b = Buf(name)

    def __getitem__(self, k):
        return self.t[k]


class FW:
    def __init__(self, nc, stack, n_dma_sems=16):
        self.nc = nc
        self.sems = {}
        self.engs = {}
        for name, h in (("pe", nc.tensor), ("dve", nc.vector), ("act", nc.scalar),
                        ("pool", nc.gpsimd), ("sp", nc.sync)):
            s = stack.enter_context(nc.semaphore("s_" + name))
            self.sems[name] = s
            self.engs[name] = Eng(name, h, s)
        self.dma_slots = []
        for i in range(n_dma_sems):
            s = stack.enter_context(nc.semaphore("s_dma%d" % i))
            key = "dma%d" % i
            self.sems[key] = s
            self.dma_slots.append([key, 0])
        self.dma_rr = 0
        self.n_inst = 0
        self.uid = 0

    def alloc(self, stack, name, shape, dtype, psum=False):
        self.uid += 1
        nm = "%s_%d" % (name, self.uid)
        if psum:
            t = stack.enter_context(self.nc.psum_tensor(nm, shape, dtype))
        else:
            t = stack.enter_context(self.nc.sbuf_tensor(nm, shape, dtype))
        return TT(t, nm)

    def _wait(self, eng, key, val):
        if val <= 0 or eng.seen.get(key, 0) >= val:
            return
        eng.h.wait_ge(self.sems[key], val)
        eng.seen[key] = val

    @staticmethod
    def _bufs(lst):
        return [x.b if isinstance(x, TT) else x for x in lst]

    def _need(self, reads, writes):
        need = {}
        for b in reads:
            for k, v in b.w.items():
                if need.get(k, 0) < v:
                    need[k] = v
        for b in writes:
            for k, v in b.w.items():
                if need.get(k, 0) < v:
                    need[k] = v
            for k, v in b.r.items():
                if need.get(k, 0) < v:
                    need[k] = v
        return need

    def _mark(self, ev, reads, writes):
        for b in reads:
            if b.r.get(ev[0], 0) < ev[1]:
                b.r[ev[0]] = ev[1]
        for b in writes:
            b.w = {ev[0]: ev[1]}
            b.r = {}
        self.n_inst += 1

    def op(self, engname, fn, reads=(), writes=()):
        eng = self.engs[engname]
        reads = self._bufs(reads)
        writes = self._bufs(writes)
        need = self._need(reads, writes)
        for k, v in need.items():
            if k == eng.name:
                if k == "pe":
                    continue
                self._wait(eng, k, v)
                continue
            self._wait(eng, k, v)
        inst = fn(eng.h)
        eng.count += 1
        inst.then_inc(eng.sem, 1)
        self._mark((eng.name, eng.count), reads, writes)

    def dma(self, out, in_, reads=(), writes=(), q="sp", **kw):
        eng = self.engs[q]
        reads = self._bufs(reads)
        writes = self._bufs(writes)
        need = self._need(reads, writes)
        for k, v in need.items():
            self._wait(eng, k, v)
        slot = self.dma_slots[self.dma_rr]
        self.dma_rr = (self.dma_rr + 1) % len(self.dma_slots)
        self._wait(eng, slot[0], slot[1])
        inst = eng.h.dma_start(out=out, in_=in_, **kw)
        slot[1] += 16
        inst.then_inc(self.sems[slot[0]], 16)
        self._mark((slot[0], slot[1]), reads, writes)

    def barrier(self):
        for e in self.engs.values():
            for f in self.engs.values():
                if f is not e and f.count > 0:
                    self._wait(e, f.name, f.count)
            for key, val in self.dma_slots:
                self._wait(e, key, val)

    def mm(self, out, lhsT, rhs, start, reads=(), writes=()):
        self.op("pe", lambda e: e.matmul(out, lhsT, rhs, start=start, stop=True,
                                         skip_group_check=True), reads, writes)

    def tr(self, out, in_, ident, reads=(), writes=()):
        self.op("pe", lambda e: e.transpose(out, in_, ident), reads, writes)

    def act(self, out, in_, func, reads=(), writes=(), **kw):
        self.op("act", lambda e: e.activation(out, in_, func, **kw), reads, writes)

    def tt(self, eng, out, a, b, op, reads=(), writes=()):
        self.op(eng, lambda e: e.tensor_tensor(out, a, b, op), reads, writes)

    def ts(self, eng, out, a, s1, s2, op0, op1=None, reads=(), writes=()):
        if op1 is None:
            self.op(eng, lambda e: e.tensor_scalar(out, a, s1, None, op0), reads, writes)
        else:
            self.op(eng, lambda e: e.tensor_scalar(out, a, s1, s2, op0, op1), reads, writes)

    def stt(self, out, a, s, b, op0, op1, reads=(), writes=()):
        self.op("dve", lambda e: e.scalar_tensor_tensor(out, a, s, b, op0, op1), reads, writes)

    def cp(self, eng, out, in_, reads=(), writes=()):
        if eng == "act":
            self.op("act", lambda e: e.copy(out, in_), reads, writes)
        else:
            self.op(eng, lambda e: e.tensor_copy(out, in_), reads, writes)

    def memset(self, eng, ap, val, writes=()):
        self.op(eng, lambda e: e.memset(ap, val), (), writes)


class Ring:
    def __init__(self, items):
        self.items = items
        self.i = 0

    def next(self):
        x = self.items[self.i]
        self.i = (self.i + 1) % len(self.items)
        return x


class TilePool:
    def __init__(self, fw, stack, depth=2):
        self.fw = fw
        self.stack = stack
        self.depth = depth
        self.rings = {}

    def get(self, name, shape, dt=F32):
        if name not in self.rings:
            self.rings[name] = Ring([self.fw.alloc(self.stack, name, shape, dt) for _ in range(self.depth)])
        return self.rings[name].next()


class _Stop(Exception):
    pass


def build_program(NQ, paired=True, final_norm=True, stop=99, dbg=False):
    nc = bass.Bass("TRN2", target_bir_lowering=False)

    def din(name, shape, dt=F32):
        return nc.dram_tensor(name, list(shape), dt, kind="ExternalInput").ap()

    xs = din("xs", [SEQ, D])
    xo = din("xo", [NQ * P, D])
    po = din("po", [NQ * P, 256])
    pos_t = din("pos_t", [P, NB + NQ], I32)
    w_kv = din("w_kv", [D, NKV])
    w_q = din("w_q", [D, NQC])
    norm_g = din("norm_g", [P, KC])
    ple_g = din("ple_g", [P, KC])
    lam_in = din("lam_in", [P, 256])
    lam_init = din("lam_init", [P, 1])
    subln = din("subln", [P, 128])
    fin_g = din("fin_g", [P, D])
    peT_k = din("peT_k", [64, 32])
    peT_v = din("peT_v", [64, 32])
    w1_k = din("w1_k", [64, 32, 128])
    w1_v = din("w1_v", [64, 32, 128])
    w2_k = din("w2_k", [128, 64])
    w2_v = din("w2_v", [128, 64])
    ovl = din("ovl", [P, 2, 64])
    w_up = din("w_up", [P, 12, D])
    w_out = din("w_out", [P, KC, D])
    w_ple = din("w_ple", [P, 2, D])
    w_pg = din("w_pg", [P, KC, D])
    rmT_in = din("rmT", [P, 2, P])
    rmQ_in = din("rmQ", [P, 2, P])
    wmT_in = din("wmT", [P, 6, P])
    cmask_in = din("cmask", [NQ, P, 2, P])
    slcka_in = din("slcka", [NQ, P, 2, 64])
    x_out = nc.dram_tensor("x_out", [NQ * P, D], F32, kind="ExternalOutput").ap()
    y_out = nc.dram_tensor("y_out", [NQ * P, D], F32, kind="ExternalOutput").ap()
    Zkv = nc.dram_tensor("Zkv", [SEQ, NKV], F32, kind="ExternalOutput" if dbg else "Internal").ap()
    Zq = nc.dram_tensor("Zq", [NQ * P, NQC], F32, kind="ExternalOutput" if dbg else "Internal").ap()
    bZkv = [Buf("zkv%d" % i) for i in range(NB)]
    bZq = [Buf("zq%d" % i) for i in range(NQ)]
    bOut = Buf("out")
    if dbg:
        dbg_tab = nc.dram_tensor("dbg_tab", [2, P, NB + NQ, 8], F32, kind="ExternalOutput").ap()
        dbg_kc = nc.dram_tensor("dbg_kc", [64, 2, 256], F32, kind="ExternalOutput").ap()
        dbg_vcx = nc.dram_tensor("dbg_vcx", [P, 2, 2, 129], F32, kind="ExternalOutput").ap()
        dbg_brt = nc.dram_tensor("dbg_brt", [3, P, 4, NQ * P], BF16, kind="ExternalOutput").ap()

    if paired:
        def kstruct(j):
            return [(kb, (kb - 2 * j) if kb >= 2 * j else None) for kb in range(2 * j + 2)]

        def wstruct(j):
            return [(2 * j - 4 + m, m if m in (0, 1, 4, 5) else None) for m in range(6)
                    if 2 * j - 4 + m >= 0]
        def qmin(j):
            return 2 * j
    else:
        def kstruct(j):
            return [(kb, 0 if kb == j else None) for kb in range(j + 1)]

        def wstruct(j):
            return [(j - 4 + m, {0: 0, 4: 4}.get(m)) for m in range(5) if j - 4 + m >= 0]
        def qmin(j):
            return j

    try:
      with ExitStack() as top:
        fw = FW(nc, top)

        def checkpoint(k):
            if stop == k:
                fw.barrier()
                print("STOP at", k, "instructions:", fw.n_inst)
                raise _Stop()

        A = lambda st, name, shape, dt=F32: fw.alloc(st, name, shape, dt)
        ps = fw.alloc(top, "ps", [P, 4096], F32, psum=True)
        bank = [Buf("bank%d" % i) for i in range(8)]

        def bk(i, c0=0, c1=512):
            return ps.t[:, i * 512 + c0:i * 512 + c1]

        def bkb(i):
            return ps.t[:, i * 512:(i + 1) * 512].bitcast(BF16)

        idb = A(top, "idb", [P, P], BF16)
        idf = A(top, "idf", [P, P], F32)
        id4 = A(top, "id4", [P, 4, P], BF16)
        epsT = A(top, "eps", [P, 1])
        fw.memset("pool", idf[:], 1.0, [idf])
        fw.op("pool", lambda e: e.affine_select(idf[:], idf[:], [[1, P]], ALU.is_equal, 0.0,
                                                base=0, channel_multiplier=-1), [idf], [idf])
        fw.cp("dve", idb[:], idf[:], [idf], [idb])
        for i in range(4):
            fw.cp("dve", id4[:, i, :], idf[:], [idf], [id4])
        fw.memset("dve", epsT[:], EPS, [epsT])

        def load_const(name, src, shape, dt=F32, st=top):
            t = A(st, name, shape, dt)
            fw.dma(t[:], src, (), [t])
            return t

        ng = load_const("ng", norm_g, [P, KC])
        pg = load_const("pg", ple_g, [P, KC])
        subg = load_const("subg", subln, [P, 128])
        lamI = load_const("lamI", lam_init, [P, 1])
        rmTf = load_const("rmTf", rmT_in, [P, 2, P])
        rmQ = load_const("rmQ", rmQ_in, [P, 2, P])
        wmTf = load_const("wmTf", wmT_in, [P, 6, P])
        rmT = A(top, "rmT", [P, 2, P], BF16)
        wmT = A(top, "wmT", [P, 6, P], BF16)
        fw.cp("dve", rmT[:], rmTf[:], [rmTf], [rmT])
        fw.cp("dve", wmT[:], wmTf[:], [wmTf], [wmT])

        checkpoint(-3)
        lamT = A(top, "lamT", [P, 1])
        nlam = A(top, "nlam", [P, 1])
        oml = A(top, "oml", [P, 1])
        with ExitStack() as st:
            lt = load_const("lt", lam_in, [P, 256], st=st)
            pr = A(st, "pr", [P, 2, 64])
            sm = A(st, "sm", [P, 2])
            ex = A(st, "ex", [P, 2])
            l3 = lt[:].rearrange("p (a d) -> p a d", a=4)
            fw.tt("dve", pr[:, 0, :], l3[:, 0, :], l3[:, 1, :], ALU.mult, [lt], [pr])
            fw.tt("dve", pr[:, 1, :], l3[:, 2, :], l3[:, 3, :], ALU.mult, [lt], [pr])
            fw.op("dve", lambda e: e.tensor_reduce(sm[:], pr[:], AX.X, ALU.add), [pr], [sm])
            fw.act(ex[:], sm[:], AF.Exp, [sm], [ex])
            fw.tt("dve", lamT[:], ex[:, 0:1], ex[:, 1:2], ALU.subtract, [ex], [lamT])
            fw.tt("dve", lamT[:], lamT[:], lamI[:], ALU.add, [lamT, lamI], [lamT])
            fw.ts("dve", nlam[:], lamT[:], -1.0, None, ALU.mult, None, [lamT], [nlam])
            fw.ts("dve", oml[:], lamI[:], -1.0, 1.0, ALU.mult, ALU.add, [lamI], [oml])
            fw.barrier()

        checkpoint(-2)
        NT = NB + NQ
        cosT = A(top, "cosT", [P, NT, 8])
        sinT = A(top, "sinT", [P, NT, 8])
        with ExitStack() as st:
            pi_ = load_const("posi", pos_t, [P, NT], I32, st=st)
            pf = A(st, "posf", [P, NT])
            ang = A(st, "ang", [P, NT, 8])
            kf = A(st, "kf", [P, NT, 8])
            ki = A(st, "ki", [P, NT, 8], I32)
            r = A(st, "r", [P, NT, 8])
            m = A(st, "m", [P, NT, 8])
            rc = A(st, "rc", [P, NT, 8])
            fw.cp("dve", pf[:], pi_[:], [pi_], [pf])
            for i in range(8):
                inv = 500000.0 ** (-(2.0 * i) / 16.0)
                fw.ts("dve", ang[:, :, i], pf[:], float(np.float32(inv)), None, ALU.mult, None,
                      [pf], [ang])
            TWO_PI = 2.0 * math.pi
            c1 = float(np.float32(TWO_PI))
            c2 = TWO_PI - c1
            MAGIC = 12582912.0
            fw.ts("dve", kf[:], ang[:], 1.0 / TWO_PI, MAGIC, ALU.mult, ALU.add, [ang], [kf])
            fw.ts("dve", kf[:], kf[:], -MAGIC, None, ALU.add, None, [kf], [kf])
            fw.stt(r[:], kf[:], -c1, ang[:], ALU.mult, ALU.add, [kf, ang], [r])
            fw.stt(r[:], kf[:], -c2, r[:], ALU.mult, ALU.add, [kf, r], [r])
            fw.ts("dve", m[:], r[:], math.pi, -TWO_PI, ALU.is_gt, ALU.mult, [r], [m])
            fw.tt("dve", r[:], r[:], m[:], ALU.add, [r, m], [r])
            fw.ts("dve", m[:], r[:], -math.pi, TWO_PI, ALU.is_lt, ALU.mult, [r], [m])
            fw.tt("dve", r[:], r[:], m[:], ALU.add, [r, m], [r])
            checkpoint(-1)
            q = rc
            u = kf
            pp = m
            sq = A(st, "sq", [P, NT, 8])
            cq = A(st, "cq", [P, NT, 8])
            fw.ts("dve", q[:], r[:], 0.5, None, ALU.mult, None, [r], [q])
            fw.tt("dve", u[:], q[:], q[:], ALU.mult, [q], [u])
            sa = [-1.0 / 6, 1.0 / 120, -1.0 / 5040, 1.0 / 362880, -1.0 / 39916800]
            fw.ts("dve", pp[:], u[:], sa[4], None, ALU.mult, None, [u], [pp])
            for kk in (3, 2, 1, 0):
                fw.stt(pp[:], pp[:], sa[kk], u[:], ALU.add, ALU.mult, [pp, u], [pp])
            fw.stt(sq[:], pp[:], 1.0, q[:], ALU.add, ALU.mult, [pp, q], [sq])
            ca = [-0.5, 1.0 / 24, -1.0 / 720, 1.0 / 40320, -1.0 / 3628800, 1.0 / 479001600]
            fw.ts("dve", pp[:], u[:], ca[5], None, ALU.mult, None, [u], [pp])
            for kk in (4, 3, 2, 1, 0):
                fw.stt(pp[:], pp[:], ca[kk], u[:], ALU.add, ALU.mult, [pp, u], [pp])
            fw.ts("dve", cq[:], pp[:], 1.0, None, ALU.add, None, [pp], [cq])
            fw.stt(sinT[:], sq[:], 2.0, cq[:], ALU.mult, ALU.mult, [sq, cq], [sinT])
            fw.tt("dve", pp[:], sq[:], sq[:], ALU.mult, [sq], [pp])
            fw.ts("dve", cosT[:], pp[:], -2.0, 1.0, ALU.mult, ALU.add, [pp], [cosT])
            fw.barrier()
        bTab = [cosT, sinT]
        if dbg:
            fw.dma(dbg_tab[0], cosT[:], [cosT], [bOut])
            fw.dma(dbg_tab[1], sinT[:], [sinT], [bOut])
        checkpoint(0)

        def rope(st_tiles, src3, dst3, nh, tb, reads, writes):
            t1, t2, t3, t4 = st_tiles
            cb = cosT[:, tb:tb + 1, :].to_broadcast([P, nh, 8])
            sb = sinT[:, tb:tb + 1, :].to_broadcast([P, nh, 8])
            x1 = src3[:, :, 0:8]
            x2 = src3[:, :, 8:16]
            fw.tt("dve", t1[:, 0:nh, :], x1, cb, ALU.mult, reads + bTab, [t1])
            fw.tt("dve", t2[:, 0:nh, :], x2, sb, ALU.mult, reads + bTab, [t2])
            fw.tt("dve", t3[:, 0:nh, :], x2, cb, ALU.mult, reads + bTab, [t3])
            fw.tt("dve", t4[:, 0:nh, :], x1, sb, ALU.mult, reads + bTab, [t4])
            fw.tt("dve", dst3[:, :, 0:8], t1[:, 0:nh, :], t2[:, 0:nh, :], ALU.subtract, [t1, t2], writes)
            fw.tt("dve", dst3[:, :, 8:16], t3[:, 0:nh, :], t4[:, 0:nh, :], ALU.add, [t3, t4], writes)
            fw.cp("pool", dst3[:, :, 16:64], src3[:, :, 16:64], reads, writes)

        with ExitStack() as st:
            hT_s = A(st, "hT_s", [P, KC, SEQ], BF16)
            hT_o = A(st, "hT_o", [P, KC, NQ * P], BF16)
            xr = Ring([A(st, "xt", [P, D]) for _ in range(2)])
            hbr = Ring([A(st, "hb", [P, D], BF16) for _ in range(2)])
            junk = A(st, "junk", [P, D], BF16)
            ssr = Ring([A(st, "ss", [P, 1]) for _ in range(2)])
            rsr = Ring([A(st, "rs", [P, 1]) for _ in range(2)])
            tbanks = Ring([6, 7])
            cnt = 0
            for (src, nb, hT) in ((xs, NB, hT_s), (xo, NQ, hT_o)):
                for blk in range(nb):
                    xt = xr.next(); hb = hbr.next(); ss = ssr.next(); rs = rsr.next()
                    fw.dma(xt[:], src[blk * P:(blk + 1) * P, :], (), [xt])
                    fw.act(junk[:], xt[:], AF.Square, [xt], [junk, ss], accum_out=ss[:])
                    fw.act(rs[:], ss[:], AF.Sqrt, [ss, epsT], [rs], scale=1.0 / D, bias=epsT[:])
                    fw.op("dve", lambda e: e.reciprocal(rs[:], rs[:]), [rs], [rs])
                    fw.ts("dve", hb[:], xt[:], rs[:], None, ALU.mult, None, [xt, rs], [hb])
                    tb = tbanks.next()
                    for c in range(KC):
                        fw.tr(bkb(tb)[:, c * P:(c + 1) * P], hb[:, c * P:(c + 1) * P], idb[:],
                              [hb, idb], [bank[tb]])
                    eng = "act" if cnt % 2 == 0 else "dve"
                    cnt += 1
                    fw.cp(eng, hT[:, :, blk * P:(blk + 1) * P],
                          bkb(tb).rearrange("p (c t) -> p c t", c=KC), [bank[tb]], [hT])
            wfr = Ring([A(st, "wf", [P, KC, 512]) for _ in range(1)])
            wbr = Ring([A(st, "wb", [P, KC, 512], BF16) for _ in range(2)])
            zsr = Ring([A(st, "zs", [P, 512]) for _ in range(4)])
            pbanks = Ring([0, 1, 2, 3, 4, 5])
            cnt = 0
            for (W, ncols, hT, nb, Z, bZ) in ((w_kv, NKV, hT_s, NB, Zkv, bZkv),
                                             (w_q, NQC, hT_o, NQ, Zq, bZq)):
                Wv = W.rearrange("(c p) n -> p c n", p=P)
                for c0 in range(0, ncols, 512):
                    w = min(512, ncols - c0)
                    wf = wfr.next(); wb = wbr.next()
                    fw.dma(wf[:, :, 0:w], Wv[:, :, c0:c0 + w], (), [wf])
                    for c in range(KC):
                        eng = "pool" if c % 2 == 0 else "dve"
                        fw.ts(eng, wb[:, c, 0:w], wf[:, c, 0:w], ng[:, c:c + 1], None, ALU.mult, None,
                              [wf, ng], [wb])
                    for blk in range(nb):
                        b = pbanks.next()
                        for c in range(KC):
                            fw.mm(bk(b, 0, w), hT[:, c, blk * P:(blk + 1) * P], wb[:, c, 0:w],
                                  c == 0, [hT, wb], [bank[b]])
                        zs = zsr.next()
                        eng = "act" if cnt % 2 == 0 else "dve"
                        cnt += 1
                        fw.cp(eng, zs[:, 0:w], bk(b, 0, w), [bank[b]], [zs])
                        fw.dma(Z[blk * P:(blk + 1) * P, c0:c0 + w], zs[:, 0:w], [zs], [bZ[blk]])
            fw.barrier()

        checkpoint(1)
        kcmpT = A(top, "kcmpT", [64, 2, 256])
        VCX = A(top, "VCX", [P, 2, 2, 129])
        with ExitStack() as st:
            kcT = A(st, "kcT", [64, 4, SEQ])
            lr = Ring([A(st, "cl", [P, 256]) for _ in range(2)])
            c0 = KVP["c_kc"][0]
            for blk in range(NB):
                t = lr.next()
                fw.dma(t[:], Zkv[blk * P:(blk + 1) * P, c0:c0 + 256], [bZkv[blk]], [t])
                tb = 6 + blk % 2
                for i in range(4):
                    fw.tr(bk(tb)[0:64, i * P:(i + 1) * P], t[:, i * 64:(i + 1) * 64], idf[:],
                          [t, idf], [bank[tb]])
                fw.cp("dve" if blk % 2 else "act", kcT[:, :, blk * P:(blk + 1) * P],
                      bk(tb)[0:64, :].rearrange("p (i t) -> p i t", i=4), [bank[tb]], [kcT])
            fw.memset("pool", kcmpT[:], 0.0, [kcmpT])
            fw.memset("pool", VCX[:], 0.0, [VCX])
            fw.barrier()
            fw.memset("pool", VCX[:, :, :, 64:65], 1.0, [VCX])
            ovt = load_const("ovt", ovl, [P, 2, 64], st=st)
            for g in range(2):
                fw.cp("dve", VCX[:, :, g, 65:129], ovt[:], [ovt], [VCX])
            for kind, (w1d, w2d, ped) in enumerate(((w1_k, w2_k, peT_k), (w1_v, w2_v, peT_v))):
                w1 = load_const("w1", w1d, [64, 32, 128], st=st)
                w2 = load_const("w2", w2d, [128, 64], st=st)
                pe = load_const("pe", ped, [64, 32], st=st)
                cb = A(st, "cb", [P, 1])
                for l in range(32):
                    fw.mm(bk(5, 0, 1), w1[:, l, :], pe[:, l:l + 1], l == 0, [w1, pe], [bank[5]])
                fw.cp("dve", cb[:], bk(5, 0, 1), [bank[5]], [cb])
                for g in range(2):
                    idx = kind * 2 + g
                    for l in range(32):
                        fw.mm(bk(4, 0, 255), w1[:, l, :], kcT[:, idx, l:l + 16 * 254 + 1:16], l == 0,
                              [w1, kcT], [bank[4]])
                    h = A(st, "h", [P, 255]); h2 = A(st, "h2", [P, 255]); sg = A(st, "sg", [P, 255])
                    fw.act(h[:], bk(4, 0, 255), AF.Identity, [bank[4], cb], [h], bias=cb[:])
                    fw.tt("dve", h2[:], h[:], h[:], ALU.mult, [h], [h2])
                    fw.ts("dve", h2[:], h2[:], 0.044715, 1.0, ALU.mult, ALU.add, [h2], [h2])
                    fw.tt("dve", h2[:], h2[:], h[:], ALU.mult, [h2, h], [h2])
                    fw.act(sg[:], h2[:], AF.Sigmoid, [h2], [sg], scale=1.5957691216057308)
                    fw.tt("dve", h[:], h[:], sg[:], ALU.mult, [h, sg], [h])
                    if kind == 0:
                        fw.mm(bk(5, 0, 255)[0:64, :], w2[:], h[:], True, [w2, h], [bank[5]])
                        fw.cp("dve", kcmpT[:, g, 0:255], bk(5, 0, 255)[0:64, :], [bank[5]], [kcmpT])
                    else:
                        for cc in range(2):
                            sz = 128 if cc == 0 else 127
                            fw.mm(bk(5, 0, 64)[0:sz, :], h[:, cc * 128:cc * 128 + sz], w2[:], True,
                                  [w2, h], [bank[5]])
                            fw.cp("dve", VCX[0:sz, cc, g, 0:64], bk(5, 0, 64)[0:sz, :], [bank[5]], [VCX])
            fw.barrier()

        if dbg:
            fw.dma(dbg_kc, kcmpT[:], [kcmpT], [bOut])
            fw.dma(dbg_vcx, VCX[:], [VCX], [bOut])
        checkpoint(2)
        BRT = [A(top, "BRT%d" % n, [P, 4, NQ * P], BF16) for n in range(3)]
        ropeT = [A(top, "rt%d" % i, [P, 8, 8]) for i in range(4)]

        def attend(kbs, score_fn, mask_fn, pv_fn, ptr, sets, exp_scale=1.0):
            fresh = {}
            prev = None
            for step in range(len(kbs) + 1):
                cur = None
                if step < len(kbs):
                    kb = kbs[step]
                    s = sets[step % 2]
                    started = set()
                    for (half, c0_, ncol, lhsT, rhs, rd) in score_fn(kb) + mask_fn(kb):
                        b = s[half]
                        fw.mm(bk(b, c0_, c0_ + ncol), lhsT, rhs, b not in started, rd, [bank[b]])
                        started.add(b)
                    pt = ptr.next()
                    fw.act(pt[:, 0:512], bk(s[0]), AF.Exp, [bank[s[0]]], [pt], scale=exp_scale)
                    fw.act(pt[:, 512:1024], bk(s[1]), AF.Exp, [bank[s[1]]], [pt], scale=exp_scale)
                    cur = (kb, pt)
                if prev is not None:
                    kbp, ptp = prev
                    for (out_ap, ab, m_, rhs, rd) in pv_fn(kbp):
                        fw.mm(out_ap, ptp[:, m_ * P:(m_ + 1) * P], rhs, ab not in fresh,
                              [ptp] + rd, [bank[ab]])
                        fresh[ab] = 1
                prev = cur

        def branch_out(n, j, o_ap, gcols, st, reads):
            gt = st.get("gt", [P, 512]); sg = st.get("sgt", [P, 512]); ob = st.get("ob", [P, 512], BF16)
            fw.dma(gt[:], Zq[j * P:(j + 1) * P, gcols[0]:gcols[1]], [bZq[j]], [gt])
            fw.act(sg[:], gt[:], AF.Silu, [gt], [sg])
            fw.tt("dve", ob[:], o_ap, sg[:], ALU.mult, reads + [sg], [ob])
            for c in range(4):
                fw.tr(bkb(7)[:, c * P:(c + 1) * P], ob[:, c * P:(c + 1) * P], idb[:], [ob, idb], [bank[7]])
            fw.cp("dve", BRT[n][:, :, j * P:(j + 1) * P],
                  bkb(7)[:, 0:512].rearrange("p (c t) -> p c t", c=4), [bank[7]], [BRT[n]])

        def q_prep(st, j, cols, nh, pair, scale, dst, dt=BF16, rot=True, tag="q"):
            zq = st.get("zq" + tag, [P, nh * 64])
            fw.dma(zq[:], Zq[j * P:(j + 1) * P, cols[0]:cols[1]], [bZq[j]], [zq])
            qb = st.get("qb" + tag, [P, nh, 64], dt)
            z3 = zq[:].rearrange("p (h d) -> p h d", h=nh)
            if rot:
                rope(ropeT, z3, qb[:], nh, NB + j, [zq], [qb])
            else:
                fw.cp("dve", qb[:], z3, [zq], [qb])
            ident = idb if dt == BF16 else idf
            if dt == BF16:
                view = bkb(7)
            else:
                view = bk(7)
            if pair:
                for c in range(nh // 2):
                    fw.tr(view[:, c * P:(c + 1) * P], qb[:, 2 * c:2 * c + 2, :].rearrange("p a d -> p (a d)"),
                          ident[:], [qb, ident], [bank[7]])
                v3 = view[:, 0:nh // 2 * P].rearrange("p (c t) -> p c t", c=nh // 2)
                fw.ts("dve", dst[0:64, 0:nh:2, :], v3[0:64], scale, None, ALU.mult, None, [bank[7]], [dst])
                fw.ts("dve", dst[64:128, 1:nh:2, :], v3[64:128], scale, None, ALU.mult, None, [bank[7]], [dst])
            else:
                per = 8 if dt == BF16 else 4
                for h0 in range(0, nh, per):
                    for h in range(h0, min(nh, h0 + per)):
                        fw.tr(view[0:64, (h - h0) * P:(h - h0 + 1) * P], qb[:, h, :], ident[:],
                              [qb, ident], [bank[7]])
                    n_ = min(nh, h0 + per) - h0
                    fw.ts("dve", dst[0:64, h0:h0 + n_, :],
                          view[0:64, 0:n_ * P].rearrange("p (c t) -> p c t", c=n_),
                          scale, None, ALU.mult, None, [bank[7]], [dst])

        with ExitStack() as st:
            KA_T = A(st, "KA_T", [P, 4, SEQ], BF16)
            VA = A(st, "VA", [P, NB, 4, 129], BF16)
            fw.memset("pool", VA[:], 1.0, [VA])
            zr = Ring([A(st, "zka", [P, 512]) for _ in range(2)])
            zvr = Ring([A(st, "zva", [P, 512]) for _ in range(2)])
            kbr = Ring([A(st, "kbf", [P, 8, 64], BF16) for _ in range(2)])
            ck = KVP["a_k"]; cv = KVP["a_v"]
            for blk in range(NB):
                zk = zr.next(); zv = zvr.next(); kb_ = kbr.next()
                fw.dma(zk[:], Zkv[blk * P:(blk + 1) * P, ck[0]:ck[1]], [bZkv[blk]], [zk])
                fw.dma(zv[:], Zkv[blk * P:(blk + 1) * P, cv[0]:cv[1]], [bZkv[blk]], [zv])
                rope(ropeT, zk[:].rearrange("p (h d) -> p h d", h=8), kb_[:], 8, blk, [zk], [kb_])
                tb = 6 + blk % 2
                for c in range(4):
                    fw.tr(bkb(tb)[:, c * P:(c + 1) * P], kb_[:, 2 * c:2 * c + 2, :].rearrange("p a d -> p (a d)"),
                          idb[:], [kb_, idb], [bank[tb]])
                fw.cp("act", KA_T[:, :, blk * P:(blk + 1) * P],
                      bkb(tb)[:, 0:512].rearrange("p (c t) -> p c t", c=4), [bank[tb]], [KA_T])
                fw.cp("pool", VA[:, blk, :, 0:128], zv[:].rearrange("p (h d) -> p h d", h=4), [zv], [VA])
            ptr = Ring([A(st, "pt", [P, 1024], BF16) for _ in range(3)])
            checkpoint(2.1)
            sj = TilePool(fw, st)
            qzr = Ring([A(st, "QAz", [P, 8, P], BF16) for _ in range(2)])
            for t_ in qzr.items:
                fw.memset("pool", t_[:], 0.0, [t_])
            for j in range(NQ):
                if True:
                    QA_T = qzr.next()
                    q_prep(sj, j, QP["a_q"], 8, True, 0.125, QA_T)
                    checkpoint(2.2)

                    def score_fn(kb):
                        out = []
                        for m_ in range(8):
                            pr_, hf = m_ // 2, m_ % 2
                            out.append((m_ // 4, (m_ % 4) * P, P,
                                        KA_T[:, pr_, kb * P:(kb + 1) * P],
                                        QA_T[:, m_, :], [KA_T, QA_T]))
                        return out
                    ks = dict(kstruct(j))

                    def mask_fn(kb):
                        mi = ks[kb]
                        if mi is None:
                            return []
                        rhs = rmT[:, mi:mi + 1, :].to_broadcast([P, 4, P])
                        return [(hlf, 0, 512, idb[:], rhs, [idb, rmT]) for hlf in range(2)]

                    def accap(m_):
                        b = 4 + m_ // 3
                        c = (m_ % 3) * 129
                        return b, c

                    def pv_fn(kb):
                        out = []
                        for m_ in range(8):
                            b, c = accap(m_)
                            out.append((bk(b, c, c + 129), b, m_, VA[:, kb, m_ // 2, :], [VA]))
                        return out
                    attend([kb for kb, _ in kstruct(j)], score_fn, mask_fn, pv_fn, ptr,
                           [(0, 1), (2, 3)])
                    checkpoint(2.3)
                    den = sj.get("den", [P, 8]); rd_ = sj.get("rd", [P, 8])
                    for m_ in range(8):
                        b, c = accap(m_)
                        fw.cp("dve", den[:, m_:m_ + 1], bk(b, c + 128, c + 129), [bank[b]], [den])
                    fw.op("dve", lambda e: e.reciprocal(rd_[:], den[:]), [den], [rd_])
                    r3 = rd_[:].rearrange("p (h c) -> p h c", c=2)
                    fw.ts("dve", r3[:, :, 1], r3[:, :, 1], nlam[:], None, ALU.mult, None, [rd_, nlam], [rd_])
                    o = sj.get("oA", [P, 4, 128]); o2 = sj.get("o2A", [P, 128], BF16)
                    ssq = sj.get("ssq", [P, 4]); rstd = sj.get("rstdA", [P, 4])
                    for h in range(4):
                        b0, c0_ = accap(2 * h)
                        b1, c1_ = accap(2 * h + 1)
                        fw.ts("dve", o[:, h, :], bk(b0, c0_, c0_ + 128), rd_[:, 2 * h:2 * h + 1], None,
                              ALU.mult, None, [bank[b0], rd_], [o])
                        fw.stt(o[:, h, :], bk(b1, c1_, c1_ + 128), rd_[:, 2 * h + 1:2 * h + 2], o[:, h, :],
                               ALU.mult, ALU.add, [bank[b1], rd_, o], [o])
                        fw.act(o2[:], o[:, h, :], AF.Square, [o], [o2, ssq], accum_out=ssq[:, h:h + 1])
                    fw.act(rstd[:], ssq[:], AF.Sqrt, [ssq, epsT], [rstd], scale=1.0 / 128, bias=epsT[:])
                    fw.op("dve", lambda e: e.reciprocal(rstd[:], rstd[:]), [rstd], [rstd])
                    fw.ts("dve", rstd[:], rstd[:], oml[:], None, ALU.mult, None, [rstd, oml], [rstd])
                    for h in range(4):
                        fw.stt(o[:, h, :], o[:, h, :], rstd[:, h:h + 1], subg[:], ALU.mult, ALU.mult,
                               [o, rstd, subg], [o])
                    checkpoint(2.4)
                    branch_out(0, j, o[:].rearrange("p h d -> p (h d)"), QP["a_g"], sj, [o])
                    checkpoint(2.5)
            fw.barrier()

        if dbg:
            fw.dma(dbg_brt[0], BRT[0][:], [BRT[0]], [bOut])
        checkpoint(3)
        with ExitStack() as st:
            KB_T = A(st, "KB_T", [64, SEQ], BF16)
            KI_T = A(st, "KI_T", [64, SEQ], BF16)
            VB = A(st, "VB", [P, NB, 65], BF16)
            fw.memset("pool", VB[:], 1.0, [VB])
            zr = Ring([A(st, "zkb", [P, 128]) for _ in range(2)])
            zvr = Ring([A(st, "zvb", [P, 64]) for _ in range(2)])
            kbr = Ring([A(st, "kbfb", [P, 2, 64], BF16) for _ in range(2)])
            ck = (KVP["b_k"][0], KVP["i_k"][1]); cv = KVP["b_v"]
            for blk in range(NB):
                zk = zr.next(); zv = zvr.next(); kb_ = kbr.next()
                fw.dma(zk[:], Zkv[blk * P:(blk + 1) * P, ck[0]:ck[1]], [bZkv[blk]], [zk])
                fw.dma(zv[:], Zkv[blk * P:(blk + 1) * P, cv[0]:cv[1]], [bZkv[blk]], [zv])
                rope(ropeT, zk[:].rearrange("p (h d) -> p h d", h=2), kb_[:], 2, blk, [zk], [kb_])
                tb = 6 + blk % 2
                for c in range(2):
                    fw.tr(bkb(tb)[0:64, c * P:(c + 1) * P], kb_[:, c, :], idb[:], [kb_, idb], [bank[tb]])
                fw.cp("act", KB_T[:, blk * P:(blk + 1) * P], bkb(tb)[0:64, 0:P], [bank[tb]], [KB_T])
                fw.cp("act", KI_T[:, blk * P:(blk + 1) * P], bkb(tb)[0:64, P:2 * P], [bank[tb]], [KI_T])
                fw.cp("pool", VB[:, blk, 0:64], zv[:], [zv], [VB])
            ptr = Ring([A(st, "ptb", [P, 1024], BF16) for _ in range(3)])
            acc = A(st, "sacc", [P, SEQ])
            work = A(st, "swork", [P, SEQ])
            nmr = Ring([A(st, "negm", [P, SEQ], BF16) for _ in range(2)])
            rtr = Ring([A(st, "rtmp", [P, 512]) for _ in range(3)])
            m8r = Ring([A(st, "m8", [P, 8]) for _ in range(4)])
            iqr = Ring([A(st, "IQ_T", [64, 8, P], BF16) for _ in range(2)])
            iwr = Ring([A(st, "iw", [P, 8]) for _ in range(2)])
            ibanks = Ring([6, 7])

            def index_and_select(j, sj):
                nkb = len(kstruct(j))
                nk = nkb * P
                negm = nmr.next()
                IQ_T = iqr.next(); iw = iwr.next()
                q_prep(sj, j, QP["i_q"], 8, False, 1.0, IQ_T, tag="i")
                fw.dma(iw[:], Zq[j * P:(j + 1) * P, QP["i_w"][0]:QP["i_w"][1]], [bZq[j]], [iw])
                for c0_ in range(0, nk, 512):
                    w = min(512, nk - c0_)
                    for h in range(8):
                        b = ibanks.next()
                        fw.mm(bk(b, 0, w), IQ_T[:, h, :], KI_T[:, c0_:c0_ + w], True, [IQ_T, KI_T], [bank[b]])
                        if h == 0:
                            fw.ts("dve", acc[:, c0_:c0_ + w], bk(b, 0, w), 0.0, iw[:, 0:1], ALU.max, ALU.mult,
                                  [bank[b], iw], [acc])
                        else:
                            rt = rtr.next()
                            fw.act(rt[:, 0:w], bk(b, 0, w), AF.Relu, [bank[b]], [rt])
                            fw.stt(acc[:, c0_:c0_ + w], rt[:, 0:w], iw[:, h:h + 1], acc[:, c0_:c0_ + w],
                                   ALU.mult, ALU.add, [rt, iw, acc], [acc])
                for kb, mi in kstruct(j):
                    if mi is not None:
                        fw.tt("dve", acc[:, kb * P:(kb + 1) * P], acc[:, kb * P:(kb + 1) * P], rmQ[:, mi, :],
                              ALU.add, [acc, rmQ], [acc])
                if qmin(j) < 2:
                    fw.ts("dve", negm[:, 0:nk], acc[:, 0:nk], -1e8, NEGM, ALU.is_lt, ALU.mult, [acc], [negm])
                    return negm
                src = acc
                for rnd in range(32):
                    m8 = m8r.next()
                    fw.op("dve", lambda e: e.max(m8[:], src[:, 0:nk]), [src], [m8])
                    if rnd < 31:
                        fw.op("dve", lambda e: e.match_replace(work[:, 0:nk], m8[:], src[:, 0:nk], -1e30),
                              [m8, src], [work])
                        src = work
                fw.ts("dve", negm[:, 0:nk], acc[:, 0:nk], m8[:, 7:8], NEGM, ALU.is_lt, ALU.mult,
                      [acc, m8], [negm])
                return negm

            def attn_B(j, negm, sj):
                QB_T = sj.get("QB_T", [64, 8, P], BF16)
                q_prep(sj, j, QP["b_q"], 8, False, 0.125, QB_T)

                def score_fn(kb):
                    return [(hlf, 0, 512, KB_T[:, kb * P:(kb + 1) * P], QB_T[:, 4 * hlf:4 * hlf + 4, :],
                             [KB_T, QB_T]) for hlf in range(2)]

                def mask_fn(kb):
                    return [(hlf, 0, 512, negm[:, kb * P:(kb + 1) * P], id4[:], [negm, id4]) for hlf in range(2)]

                def pv_fn(kb):
                    return [(bk(4 + h // 4, (h % 4) * 65, (h % 4) * 65 + 65), 4 + h // 4, h, VB[:, kb, :], [VB])
                            for h in range(8)]
                attend([kb for kb, _ in kstruct(j)], score_fn, mask_fn, pv_fn, ptr, [(0, 1), (2, 3)])
                den = sj.get("denb", [P, 8]); o = sj.get("oB", [P, 8, 64])
                for h in range(8):
                    b = 4 + h // 4; c = (h % 4) * 65
                    fw.cp("dve", den[:, h:h + 1], bk(b, c + 64, c + 65), [bank[b]], [den])
                fw.op("dve", lambda e: e.reciprocal(den[:], den[:]), [den], [den])
                for h in range(8):
                    b = 4 + h // 4; c = (h % 4) * 65
                    fw.ts("dve", o[:, h, :], bk(b, c, c + 64), den[:, h:h + 1], None, ALU.mult, None,
                          [bank[b], den], [o])
                branch_out(1, j, o[:].rearrange("p h d -> p (h d)"), QP["b_g"], sj, [o])

            pend = None
            sjb = TilePool(fw, st)
            for j in range(NQ + 1):
                cur = None
                if j < NQ:
                    cur = (j, index_and_select(j, sjb), sjb)
                if pend is not None:
                    attn_B(pend[0], pend[1], pend[2])
                pend = cur
            fw.barrier()

        if dbg:
            fw.dma(dbg_brt[1], BRT[1][:], [BRT[1]], [bOut])
        checkpoint(4)
        with ExitStack() as st:
            KS_T = A(st, "KS_T", [64, 2, SEQ], BF16)
            Emat = A(st, "Emat", [64, SEQ], BF16)
            fw.memset("pool", Emat[:], 1.0, [Emat])
            fw.op("pool", lambda e: e.affine_select(Emat[:], Emat[:], [[1, SEQ]], ALU.is_ge, 0.0,
                                                    base=0, channel_multiplier=-64), [Emat], [Emat])
            fw.op("pool", lambda e: e.affine_select(Emat[:], Emat[:], [[-1, SEQ]], ALU.is_ge, 0.0,
                                                    base=63, channel_multiplier=64), [Emat], [Emat])
            KW_T = A(st, "KW_T", [64, 2, SEQ], BF16)
            VS = A(st, "VS", [P, NB, 2, 65], BF16)
            VW = A(st, "VW", [P, NB, 2, 65], BF16)
            fw.memset("pool", VS[:], 1.0, [VS])
            fw.memset("pool", VW[:], 1.0, [VW])
            zr = Ring([A(st, "zkc", [P, 256]) for _ in range(2)])
            zvr = Ring([A(st, "zvc", [P, 256]) for _ in range(2)])
            kbr = Ring([A(st, "kbfc", [P, 4, 64], BF16) for _ in range(2)])
            ck = (KVP["c_ks"][0], KVP["c_kw"][1]); cv = (KVP["c_vs"][0], KVP["c_vw"][1])
            for blk in range(NB):
                zk = zr.next(); zv = zvr.next(); kb_ = kbr.next()
                fw.dma(zk[:], Zkv[blk * P:(blk + 1) * P, ck[0]:ck[1]], [bZkv[blk]], [zk])
                fw.dma(zv[:], Zkv[blk * P:(blk + 1) * P, cv[0]:cv[1]], [bZkv[blk]], [zv])
                rope(ropeT, zk[:].rearrange("p (h d) -> p h d", h=4), kb_[:], 4, blk, [zk], [kb_])
                tb = 6 + blk % 2
                for c in range(4):
                    fw.tr(bkb(tb)[0:64, c * P:(c + 1) * P], kb_[:, c, :], idb[:], [kb_, idb], [bank[tb]])
                fw.cp("act", KS_T[:, :, blk * P:(blk + 1) * P],
                      bkb(tb)[0:64, 0:2 * P].rearrange("p (c t) -> p c t", c=2), [bank[tb]], [KS_T])
                fw.cp("act", KW_T[:, :, blk * P:(blk + 1) * P],
                      bkb(tb)[0:64, 2 * P:4 * P].rearrange("p (c t) -> p c t", c=2), [bank[tb]], [KW_T])
                z4 = zv[:].rearrange("p (a g d) -> p a g d", a=2, g=2)
                fw.cp("pool", VS[:, blk, :, 0:64], z4[:, 0, :, :], [zv], [VS])
                fw.cp("pool", VW[:, blk, :, 0:64], z4[:, 1, :, :], [zv], [VW])
            ptr = Ring([A(st, "ptc", [P, 1024], BF16) for _ in range(3)])
            ptf = Ring([A(st, "ptf", [P, 1024]) for _ in range(2)])
            sj = TilePool(fw, st)
            for j in range(NQ):
                if True:
                    QC_T = sj.get("QC_T", [64, 8, P])
                    QR_T = sj.get("QR_T", [64, 8, P], BF16)
                    q_prep(sj, j, QP["c_q"], 8, False, 0.125, QC_T, dt=F32, rot=False, tag="n")
                    q_prep(sj, j, QP["c_q"], 8, False, 0.125, QR_T)
                    cm = sj.get("cm", [P, 2, P])
                    fw.dma(cm[:], cmask_in[j], (), [cm])
                    ska = sj.get("ska", [P, 2, 64])
                    fw.dma(ska[:], slcka_in[j], (), [ska])
                    gz = sj.get("gz", [P, 24]); gb = sj.get("gb", [P, 24])
                    fw.dma(gz[:], Zq[j * P:(j + 1) * P, QP["c_bg"][0]:QP["c_bg"][1]], [bZq[j]], [gz])
                    fw.act(gb[:], gz[:], AF.Sigmoid, [gz], [gb])
                    gb3 = gb[:].rearrange("p (h c) -> p h c", c=3)
                    def score_c(cc):
                        return [(g, 0, 512, kcmpT[:, g, cc * P:(cc + 1) * P], QC_T[:, 4 * g:4 * g + 4, :],
                                 [kcmpT, QC_T]) for g in range(2)]

                    def mask_c(cc):
                        rhs = cm[:, cc:cc + 1, :].to_broadcast([P, 4, P])
                        return [(g, 0, 512, idf[:], rhs, [idf, cm]) for g in range(2)]

                    def accap(h):
                        return 4 + h // 3, (h % 3) * 129

                    def pv_c(cc):
                        out = []
                        for h in range(8):
                            b, c = accap(h)
                            out.append((bk(b, c, c + 129), b, h, VCX[:, cc, h // 4, :], [VCX]))
                        return out
                    attend([0, 1], score_c, mask_c, pv_c, ptf, [(0, 1), (2, 3)])
                    den = sj.get("denc", [P, 8]); ocm = sj.get("ocm", [P, 8, 64]); imp = sj.get("imp", [P, 2, 64])
                    for h in range(8):
                        b, c = accap(h)
                        fw.ts("dve", den[:, h:h + 1], bk(b, c + 64, c + 65), 1e-30, None, ALU.max, None,
                              [bank[b]], [den])
                    fw.op("dve", lambda e: e.reciprocal(den[:], den[:]), [den], [den])
                    for h in range(8):
                        b, c = accap(h)
                        fw.ts("dve", ocm[:, h, :], bk(b, c, c + 64), den[:, h:h + 1], None, ALU.mult, None,
                              [bank[b], den], [ocm])
                        g = h // 4
                        if h % 4 == 0:
                            fw.ts("dve", imp[:, g, :], bk(b, c + 65, c + 129), den[:, h:h + 1], None, ALU.mult,
                                  None, [bank[b], den], [imp])
                        else:
                            fw.stt(imp[:, g, :], bk(b, c + 65, c + 129), den[:, h:h + 1], imp[:, g, :],
                                   ALU.mult, ALU.add, [bank[b], den, imp], [imp])
                    for g in range(2):
                        fw.tt("dve", imp[:, g, :], imp[:, g, :], ska[:, 0, :], ALU.mult, [imp, ska], [imp])
                        fw.tt("dve", imp[:, g, :], imp[:, g, :], ska[:, 1, :], ALU.add, [imp, ska], [imp])
                    selm = sj.get("selm", [P, 2, 64], BF16)
                    wk = sj.get("wk", [P, 64]); ma = sj.get("ma", [P, 8]); mb = sj.get("mb", [P, 8])
                    for g in range(2):
                        fw.op("dve", lambda e: e.max(ma[:], imp[:, g, :]), [imp], [ma])
                        fw.op("dve", lambda e: e.match_replace(wk[:], ma[:], imp[:, g, :], -3e38), [ma, imp], [wk])
                        fw.op("dve", lambda e: e.max(mb[:], wk[:]), [wk], [mb])
                        fw.ts("dve", selm[:, g, :], imp[:, g, :], mb[:, 7:8], NEGM, ALU.is_lt, ALU.mult,
                              [imp, mb], [selm])
                    selT = sj.get("selT", [64, 2, P], BF16)
                    for g in range(2):
                        fw.tr(bkb(7)[0:64, g * P:(g + 1) * P], selm[:, g, :], idb[:], [selm, idb], [bank[7]])
                    fw.cp("dve", selT[:], bkb(7)[0:64, 0:2 * P].rearrange("p (c t) -> p c t", c=2), [bank[7]], [selT])
                    ks = dict(kstruct(j))

                    def score_s(kb):
                        return [(g, 0, 512, KS_T[:, g, kb * P:(kb + 1) * P], QR_T[:, 4 * g:4 * g + 4, :],
                                 [KS_T, QR_T]) for g in range(2)]

                    def mask_s(kb):
                        out = [(g, 0, 512, Emat[:, kb * P:(kb + 1) * P],
                                selT[:, g:g + 1, :].to_broadcast([64, 4, P]), [Emat, selT]) for g in range(2)]
                        mi = ks[kb]
                        if mi is not None:
                            rhs = rmT[:, mi:mi + 1, :].to_broadcast([P, 4, P])
                            out += [(g, 0, 512, idb[:], rhs, [idb, rmT]) for g in range(2)]
                        return out

                    def pv_s(kb):
                        return [(bk(4 + h // 4, (h % 4) * 65, (h % 4) * 65 + 65), 4 + h // 4, h,
                                 VS[:, kb, h // 4, :], [VS]) for h in range(8)]
                    attend([kb for kb, _ in kstruct(j)], score_s, mask_s, pv_s, ptr, [(0, 1), (2, 3)])
                    oc = sj.get("oC", [P, 8, 64]); dn2 = sj.get("dn2", [P, 8])

                    def fold(first, gi):
                        for h in range(8):
                            b = 4 + h // 4; c = (h % 4) * 65
                            fw.cp("dve", dn2[:, h:h + 1], bk(b, c + 64, c + 65), [bank[b]], [dn2])
                        fw.op("dve", lambda e: e.reciprocal(dn2[:], dn2[:]), [dn2], [dn2])
                        fw.tt("dve", dn2[:], dn2[:], gb3[:, :, gi], ALU.mult, [dn2, gb], [dn2])
                        for h in range(8):
                            b = 4 + h // 4; c = (h % 4) * 65
                            if first:
                                fw.ts("dve", oc[:, h, :], bk(b, c, c + 64), dn2[:, h:h + 1], None, ALU.mult, None,
                                      [bank[b], dn2], [oc])
                            else:
                                fw.stt(oc[:, h, :], bk(b, c, c + 64), dn2[:, h:h + 1], oc[:, h, :],
                                       ALU.mult, ALU.add, [bank[b], dn2, oc], [oc])
                    fold(True, 1)
                    wsd = dict(wstruct(j))

                    def score_w(kb):
                        return [(g, 0, 512, KW_T[:, g, kb * P:(kb + 1) * P], QR_T[:, 4 * g:4 * g + 4, :],
                                 [KW_T, QR_T]) for g in range(2)]

                    def mask_w(kb):
                        mi = wsd[kb]
                        if mi is None:
                            return []
                        rhs = wmT[:, mi:mi + 1, :].to_broadcast([P, 4, P])
                        return [(g, 0, 512, idb[:], rhs, [idb, wmT]) for g in range(2)]

                    def pv_w(kb):
                        return [(bk(4 + h // 4, (h % 4) * 65, (h % 4) * 65 + 65), 4 + h // 4, h,
                                 VW[:, kb, h // 4, :], [VW]) for h in range(8)]
                    attend([kb for kb, _ in wstruct(j)], score_w, mask_w, pv_w, ptr, [(0, 1), (2, 3)])
                    fold(False, 2)
                    for h in range(8):
                        fw.stt(oc[:, h, :], ocm[:, h, :], gb3[:, h, 0:1], oc[:, h, :], ALU.mult, ALU.add,
                               [ocm, gb, oc], [oc])
                    branch_out(2, j, oc[:].rearrange("p h d -> p (h d)"), QP["c_g"], sj, [oc])
            fw.barrier()

        if dbg:
            fw.dma(dbg_brt[2], BRT[2][:], [BRT[2]], [bOut])
        checkpoint(5)
        with ExitStack() as st:
            wup = A(st, "wup", [P, 12, D], BF16)
            wo = A(st, "wo", [P, KC, D], BF16)
            wpl = A(st, "wpl", [P, 2, D], BF16)
            wpg = A(st, "wpg", [P, KC, D], BF16)
            fg = load_const("fg", fin_g, [P, D], st=st)
            wst = Ring([A(st, "wst", [P, 2, D]) for _ in range(2)])
            i = 0
            for (src, dst, n, gain) in ((w_up, wup, 12, None), (w_out, wo, KC, None), (w_ple, wpl, 2, None),
                                        (w_pg, wpg, KC, pg)):
                for c0 in range(0, n, 2):
                    s_ = wst.next()
                    fw.dma(s_[:], src[:, c0:c0 + 2, :], (), [s_])
                    for c in range(2):
                        eng = "pool" if i % 2 == 0 else "dve"
                        i += 1
                        if gain is None:
                            fw.cp(eng, dst[:, c0 + c, :], s_[:, c, :], [s_], [dst])
                        else:
                            fw.ts(eng, dst[:, c0 + c, :], s_[:, c, :], gain[:, c0 + c:c0 + c + 1], None, ALU.mult,
                                  None, [s_, gain], [dst])
            sj = TilePool(fw, st, depth=1)
            for j in range(NQ):
                if True:
                    mg = sj.get("mg", [P, 3, D])
                    fw.dma(mg[:], Zq[j * P:(j + 1) * P, QP["merge"][0]:QP["merge"][1]].rearrange(
                        "p (n d) -> p n d", n=3), [bZq[j]], [mg])
                    fw.act(mg[:], mg[:], AF.Sigmoid, [mg], [mg])
                    xt = sj.get("xt2", [P, D]); pt_ = sj.get("pin", [P, 256]); pb = sj.get("pb", [P, 256], BF16)
                    fw.dma(xt[:], xo[j * P:(j + 1) * P, :], (), [xt])
                    fw.dma(pt_[:], po[j * P:(j + 1) * P, :], (), [pt_])
                    msum = sj.get("msum", [P, D]); tmp = sj.get("tmpm", [P, D])
                    for n in range(3):
                        for hf in range(2):
                            b = 2 * (n % 2) + hf
                            for c in range(4):
                                fw.mm(bk(b), BRT[n][:, c, j * P:(j + 1) * P], wup[:, n * 4 + c, hf * 512:(hf + 1) * 512],
                                      c == 0, [BRT[n], wup], [bank[b]])
                            dst = msum if n == 0 else tmp
                            fw.tt("dve", dst[:, hf * 512:(hf + 1) * 512], bk(b), mg[:, n, hf * 512:(hf + 1) * 512],
                                  ALU.mult, [bank[b], mg], [dst])
                        if n > 0:
                            fw.tt("pool", msum[:], msum[:], tmp[:], ALU.add, [msum, tmp], [msum])
                    mb_ = sj.get("mb_", [P, D], BF16)
                    fw.cp("act", mb_[:], msum[:], [msum], [mb_])
                    mT = sj.get("mT", [P, KC, P], BF16)
                    for c in range(KC):
                        fw.tr(bkb(7)[:, c * P:(c + 1) * P], mb_[:, c * P:(c + 1) * P], idb[:], [mb_, idb], [bank[7]])
                    fw.cp("dve", mT[:], bkb(7).rearrange("p (c t) -> p c t", c=KC), [bank[7]], [mT])
                    x1 = sj.get("x1", [P, D])
                    for hf in range(2):
                        b = 4 + hf
                        for c in range(KC):
                            fw.mm(bk(b), mT[:, c, :], wo[:, c, hf * 512:(hf + 1) * 512], c == 0, [mT, wo], [bank[b]])
                        fw.tt("dve", x1[:, hf * 512:(hf + 1) * 512], bk(b), xt[:, hf * 512:(hf + 1) * 512], ALU.add,
                              [bank[b], xt], [x1])
                    fw.cp("dve", pb[:], pt_[:], [pt_], [pb])
                    pT = sj.get("pT", [P, 2, P], BF16)
                    for c in range(2):
                        fw.tr(bkb(6)[:, c * P:(c + 1) * P], pb[:, c * P:(c + 1) * P], idb[:], [pb, idb], [bank[6]])
                    fw.cp("dve", pT[:], bkb(6)[:, 0:2 * P].rearrange("p (c t) -> p c t", c=2), [bank[6]], [pT])
                    ss = sj.get("ss2", [P, 1]); hb = sj.get("hb2", [P, D], BF16); jk = sj.get("jk2", [P, D], BF16)
                    fw.act(jk[:], x1[:], AF.Square, [x1], [jk, ss], accum_out=ss[:])
                    fw.act(ss[:], ss[:], AF.Sqrt, [ss, epsT], [ss], scale=1.0 / D, bias=epsT[:])
                    fw.op("dve", lambda e: e.reciprocal(ss[:], ss[:]), [ss], [ss])
                    fw.ts("dve", hb[:], x1[:], ss[:], None, ALU.mult, None, [x1, ss], [hb])
                    hT2 = sj.get("hT2", [P, KC, P], BF16)
                    for c in range(KC):
                        fw.tr(bkb(7)[:, c * P:(c + 1) * P], hb[:, c * P:(c + 1) * P], idb[:], [hb, idb], [bank[7]])
                    fw.cp("act", hT2[:], bkb(7).rearrange("p (c t) -> p c t", c=KC), [bank[7]], [hT2])
                    sgt = sj.get("sgt2", [P, D]); x2 = sj.get("x2", [P, D])
                    for hf in range(2):
                        b = hf
                        for c in range(KC):
                            fw.mm(bk(b), hT2[:, c, :], wpg[:, c, hf * 512:(hf + 1) * 512], c == 0, [hT2, wpg], [bank[b]])
                        fw.act(sgt[:, hf * 512:(hf + 1) * 512], bk(b), AF.Sigmoid, [bank[b]], [sgt])
                        b2 = 2 + hf
                        for c in range(2):
                            fw.mm(bk(b2), pT[:, c, :], wpl[:, c, hf * 512:(hf + 1) * 512], c == 0, [pT, wpl], [bank[b2]])
                        fw.tt("dve", sgt[:, hf * 512:(hf + 1) * 512], bk(b2), sgt[:, hf * 512:(hf + 1) * 512],
                              ALU.mult, [bank[b2], sgt], [sgt])
                    fw.tt("pool", x2[:], x1[:], sgt[:], ALU.add, [x1, sgt], [x2])
                    fw.dma(x_out[j * P:(j + 1) * P, :], x2[:], [x2], [bOut])
                    if final_norm:
                        ss3 = sj.get("ss3", [P, 1]); yo = sj.get("yo", [P, D])
                        fw.act(jk[:], x2[:], AF.Square, [x2], [jk, ss3], accum_out=ss3[:])
                        fw.act(ss3[:], ss3[:], AF.Sqrt, [ss3, epsT], [ss3], scale=1.0 / D, bias=epsT[:])
                        fw.op("dve", lambda e: e.reciprocal(ss3[:], ss3[:]), [ss3], [ss3])
                        fw.stt(yo[:], x2[:], ss3[:], fg[:], ALU.mult, ALU.mult, [x2, ss3, fg], [yo])
                        fw.dma(y_out[j * P:(j + 1) * P, :], yo[:], [yo], [bOut])
            fw.barrier()
        print("instructions:", fw.n_inst, {k: e.count for k, e in fw.engs.items()})
    except _Stop:
        pass
    return nc


def _role_masks(r):
    i = np.arange(P)
    tri = np.where(i[:, None] <= i[None, :], 0.0, NEGM).astype(np.float32)
    full = np.full((P, P), NEGM, np.float32)
    zero = np.zeros((P, P), np.float32)
    upper = np.where(i[:, None] > i[None, :], 0.0, NEGM).astype(np.float32)
    if r == 0:
        rmT = np.stack([tri, full], 1)
        wm = [upper, zero, zero, zero, tri, full]
    else:
        rmT = np.stack([zero, tri], 1)
        wm = [full, upper, zero, zero, zero, tri]
    rmQ = np.stack([np.where(rmT[:, m, :].T < 0, -1e9, 0.0) for m in range(2)], 1).astype(np.float32)
    return rmT.astype(np.float32), rmQ, np.stack(wm, 1).astype(np.float32)


def _cmp_slc_masks(r, NQ):
    cm = np.zeros((NQ, P, 2, P), np.float32)
    ska = np.zeros((NQ, P, 2, 64), np.float32)
    n = np.arange(256)
    blk = np.arange(64)
    for j in range(NQ):
        t = (2 * j + r) * P + np.arange(P)
        valid = ((16 * n[:, None] + 31) <= t[None, :]) & (n[:, None] < 255)
        m = np.where(valid, 0.0, NEGM).astype(np.float32)
        cm[j, :, 0, :] = m[:128]
        cm[j, :, 1, :] = m[128:]
        forced = (blk[None, :] == (t // 64)[:, None]) | (blk[None, :] == 0)
        causal = blk[None, :] * 64 <= t[:, None]
        keep = np.where(forced | ~causal, 0.0, 1.0)
        add = np.where(~causal, -1e30, np.where(forced, 1e4, 0.0))
        ska[j, :, 0, :] = keep
        ska[j, :, 1, :] = add
    return cm, ska


_PROG = {}


def _get_prog(NQ):
    if NQ not in _PROG:
        _PROG[NQ] = build_program(NQ)
    return _PROG[NQ]


def _bc(v, n=P):
    return np.ascontiguousarray(np.broadcast_to(np.asarray(v, np.float32).reshape(1, -1), (n, v.size)))


def _layer_inputs(i, x_full, p, positions, norm_g, w_in, diff_lambda, diff_subln_g, cmp_pe_k, cmp_w1_k,
                  cmp_w2_k, cmp_pe_v, cmp_w1_v, cmp_w2_v, w_up, w_out, ple_norm_g, w_ple, w_ple_gate,
                  final_norm_g):
    NQ = 16
    B = x_full.shape[0]
    w_kv = np.ascontiguousarray(w_in[i][:, KV_IDX])
    w_q = np.ascontiguousarray(w_in[i][:, Q_IDX])
    lam_init = 0.8 - 0.6 * math.exp(-0.3 * i)
    cstart = np.arange(255) * 16
    sstart = np.arange(64) * 64
    ov = ((cstart[:, None] < sstart[None, :] + 64) & (cstart[:, None] + 32 > sstart[None, :])).astype(np.float32)
    ov = np.concatenate([ov, np.zeros((1, 64), np.float32)], 0).reshape(2, P, 64).transpose(1, 0, 2)
    common = {
        "w_kv": w_kv, "w_q": w_q,
        "norm_g": np.ascontiguousarray(norm_g[i].reshape(KC, P).T),
        "ple_g": np.ascontiguousarray(ple_norm_g[i].reshape(KC, P).T),
        "lam_in": _bc(diff_lambda[i].reshape(-1)),
        "lam_init": np.full((P, 1), lam_init, np.float32),
        "subln": _bc(diff_subln_g[i]),
        "fin_g": _bc(final_norm_g),
        "peT_k": np.ascontiguousarray(cmp_pe_k[i].T), "peT_v": np.ascontiguousarray(cmp_pe_v[i].T),
        "w1_k": np.ascontiguousarray(cmp_w1_k[i].transpose(1, 0, 2)),
        "w1_v": np.ascontiguousarray(cmp_w1_v[i].transpose(1, 0, 2)),
        "w2_k": np.ascontiguousarray(cmp_w2_k[i]), "w2_v": np.ascontiguousarray(cmp_w2_v[i]),
        "ovl": np.ascontiguousarray(ov),
        "w_up": np.ascontiguousarray(w_up[i].reshape(3, 4, P, D).transpose(2, 0, 1, 3).reshape(P, 12, D)),
        "w_out": np.ascontiguousarray(w_out[i].reshape(KC, P, D).transpose(1, 0, 2)),
        "w_ple": np.ascontiguousarray(w_ple[i].reshape(2, P, D).transpose(1, 0, 2)),
        "w_pg": np.ascontiguousarray(w_ple_gate[i].reshape(KC, P, D).transpose(1, 0, 2)),
    }
    maps = []
    for core in range(2 * B):
        b, r = core // 2, core % 2
        rmT, rmQ, wmT = _role_masks(r)
        cm, ska = _cmp_slc_masks(r, NQ)
        own = np.arange(NQ) * 2 + r
        xb = x_full[b].reshape(NB, P, D)
        pb = p[i, b].reshape(NB, P, 256)
        posb = positions[b].reshape(NB, P)
        pos_t = np.concatenate([posb, posb[own]], 0).T.astype(np.int32)
        d = dict(common)
        d.update({
            "xs": np.ascontiguousarray(x_full[b]),
            "xo": np.ascontiguousarray(xb[own].reshape(NQ * P, D)),
            "po": np.ascontiguousarray(pb[own].reshape(NQ * P, 256)),
            "pos_t": np.ascontiguousarray(pos_t),
            "rmT": rmT, "rmQ": rmQ, "wmT": wmT, "cmask": cm, "slcka": ska,
        })
        maps.append(d)
    return maps


def kernel(**inputs):
    inp = {k: np.asarray(v) for k, v in inputs.items()}
    x = inp["x"].astype(np.float32)
    B = x.shape[0]
    NQ = 16
    nc = _get_prog(NQ)
    args = [inp[k] for k in ("p", "positions", "norm_g", "w_in", "diff_lambda", "diff_subln_g", "cmp_pe_k",
                             "cmp_w1_k", "cmp_w2_k", "cmp_pe_v", "cmp_w1_v", "cmp_w2_v", "w_up", "w_out",
                             "ple_norm_g", "w_ple", "w_ple_gate", "final_norm_g")]
    y = None
    for i in range(2):
        maps = _layer_inputs(i, x, *args)
        res = run_bass_kernel_spmd(nc, maps, core_ids=list(range(2 * B)))
        xn = np.empty_like(x)
        y = np.empty_like(x)
        for core in range(2 * B):
            b, r = core // 2, core % 2
            own = np.arange(NQ) * 2 + r
            xn[b].reshape(NB, P, D)[own] = res.results[core]["x_out"].reshape(NQ, P, D)
            y[b].reshape(NB, P, D)[own] = res.results[core]["y_out"].reshape(NQ, P, D)
        x = xn
    return y
```
